# Optimizing a Trainium2 kernel written in Bass

```python
import math
import jax
import jax.numpy as jnp
from jax import lax
import numpy as np

D_MODEL = 1024
BATCH = 4
SEQ = 8192
DEPTH = 4

N_MIXERS = 4
HEAD_DIM = 64
N_HEADS = D_MODEL // HEAD_DIM
ROPE_THETA = 10000.0
RMS_EPS = 1e-6
NEG_INF = -1e30
ATTN_QBLOCK = 128
MOBA_BLOCK = 256
MOBA_TOPK = 3
MOBA_QCHUNK = 32
MLA_HEADS = N_HEADS
MLA_Q_LORA = 3 * D_MODEL // 8
MLA_KV_LORA = D_MODEL // 4
MLA_NOPE = HEAD_DIM
MLA_ROPE = HEAD_DIM // 2
MLA_V = HEAD_DIM
DIL_GROUPS = ((128, 1), (512, 4), (2048, 16))
DIL_HEADS = N_HEADS
DIL_BLOCK = 128
SB_HEADS = N_HEADS
SB_QBLOCK = 128
D_FF = ((8 * D_MODEL // 3 + 255) // 256) * 256
CONV_WIDTH = 3

kernel_name = "hybrid_moba_mla_dilated_stickbreak_convffn"


def rms_norm(x, g):
    xf = x.astype(jnp.float32)
    y = xf * lax.rsqrt(jnp.mean(xf * xf, axis=-1, keepdims=True) + RMS_EPS)
    return (y * g.astype(jnp.float32)).astype(x.dtype)


def rope_tables(pos, dim):
    inv = ROPE_THETA ** (-jnp.arange(0, dim, 2, dtype=jnp.float32) / dim)
    ang = pos[:, None] * inv[None, :]
    return jnp.cos(ang), jnp.sin(ang)


def apply_rope(x, cos, sin):
    half = x.shape[-1] // 2
    x1, x2 = x[..., :half], x[..., half:]
    c = cos[None, :, None, :].astype(x.dtype)
    s = sin[None, :, None, :].astype(x.dtype)
    return jnp.concatenate([x1 * c - x2 * s, x2 * c + x1 * s], axis=-1)


def moba_attention(q, k, v):
    B, S, H, dh = q.shape
    scale = dh ** -0.5
    nb = -(-S // MOBA_BLOCK)
    sp = nb * MOBA_BLOCK
    padw = ((0, 0), (0, sp - S), (0, 0), (0, 0))
    qh = jnp.pad(q, padw).transpose(0, 2, 1, 3)
    kh = jnp.pad(k, padw).transpose(0, 2, 1, 3)
    vh = jnp.pad(v, padw).transpose(0, 2, 1, 3)
    kblk = kh.reshape(B, H, nb, MOBA_BLOCK, dh)
    vblk = vh.reshape(B, H, nb, MOBA_BLOCK, dh)
    kmean = jnp.mean(kblk.astype(jnp.float32), axis=3)
    topk = min(MOBA_TOPK, nb)
    bi = jnp.arange(B)[:, None, None, None]
    hi = jnp.arange(H)[None, :, None, None]
    blk_ids = jnp.arange(nb)

    def step(c):
        start = c * MOBA_QCHUNK
        blk = start // MOBA_BLOCK
        qc = lax.dynamic_slice_in_dim(qh, start, MOBA_QCHUNK, axis=2)
        qpos = start + jnp.arange(MOBA_QCHUNK)
        gate = jnp.einsum("bhqd,bhnd->bhqn", qc.astype(jnp.float32), kmean)
        gate = jnp.where(blk_ids < blk, gate, NEG_INF)
        _, gidx = lax.top_k(gate, topk)
        valid = gidx < blk
        ksel = kblk[bi, hi, gidx]
        vsel = vblk[bi, hi, gidx]
        s_sel = jnp.einsum("bhqd,bhqnkd->bhqnk", qc, ksel, preferred_element_type=jnp.float32) * scale
        s_sel = jnp.where(valid[..., None], s_sel, NEG_INF)
        s_sel = s_sel.reshape(B, H, MOBA_QCHUNK, topk * MOBA_BLOCK)
        kown = lax.dynamic_slice_in_dim(kh, blk * MOBA_BLOCK, MOBA_BLOCK, axis=2)
        vown = lax.dynamic_slice_in_dim(vh, blk * MOBA_BLOCK, MOBA_BLOCK, axis=2)
        s_own = jnp.einsum("bhqd,bhkd->bhqk", qc, kown, preferred_element_type=jnp.float32) * scale
        kpos = blk * MOBA_BLOCK + jnp.arange(MOBA_BLOCK)
        s_own = jnp.where(kpos[None, :] <= qpos[:, None], s_own, NEG_INF)
        p = jax.nn.softmax(jnp.concatenate([s_own, s_sel], axis=-1), axis=-1).astype(vh.dtype)
        p_own = p[..., :MOBA_BLOCK]
        p_sel = p[..., MOBA_BLOCK:].reshape(B, H, MOBA_QCHUNK, topk, MOBA_BLOCK)
        return (jnp.einsum("bhqk,bhkd->bhqd", p_own, vown)
                + jnp.einsum("bhqnk,bhqnkd->bhqd", p_sel, vsel))

    out = lax.map(step, jnp.arange(sp // MOBA_QCHUNK))
    out = out.transpose(1, 0, 3, 2, 4).reshape(B, sp, H, dh)
    return out[:, :S]


def moba_mixer(xn, w_qkv, w_o, cos, sin):
    B, S, _ = xn.shape
    qkv = (xn @ w_qkv).reshape(B, S, 3, N_HEADS, HEAD_DIM)
    q = apply_rope(qkv[:, :, 0], cos, sin)
    k = apply_rope(qkv[:, :, 1], cos, sin)
    o = moba_attention(q, k, qkv[:, :, 2])
    return o.reshape(B, S, N_HEADS * HEAD_DIM) @ w_o


def causal_attention_blocked(q, k, v, scale):
    B, S, H, dk = q.shape
    dv = v.shape[-1]
    nq = S // ATTN_QBLOCK
    qb = q.reshape(B, nq, ATTN_QBLOCK, H, dk).transpose(1, 0, 2, 3, 4)
    key_pos = jnp.arange(S)

    def step(args):
        i, qi = args
        s = jnp.einsum("bqhd,bkhd->bhqk", qi, k, preferred_element_type=jnp.float32) * scale
        qpos = i * ATTN_QBLOCK + jnp.arange(ATTN_QBLOCK)
        s = jnp.where(key_pos[None, :] <= qpos[:, None], s, NEG_INF)
        p = jax.nn.softmax(s, axis=-1).astype(v.dtype)
        return jnp.einsum("bhqk,bkhd->bqhd", p, v)

    out = lax.map(step, (jnp.arange(nq), qb))
    return out.transpose(1, 0, 2, 3, 4).reshape(B, S, H, dv)


def mla_mixer(xn, w_in, q_norm, w_uq, kv_norm, w_ukv, w_o, cos_r, sin_r):
    B, S, _ = xn.shape
    c = xn @ w_in
    cq = rms_norm(c[..., :MLA_Q_LORA], q_norm)
    ckv = rms_norm(c[..., MLA_Q_LORA:MLA_Q_LORA + MLA_KV_LORA], kv_norm)
    kr = c[..., MLA_Q_LORA + MLA_KV_LORA:]
    qf = (cq @ w_uq).reshape(B, S, MLA_HEADS, MLA_NOPE + MLA_ROPE)
    q_rope = apply_rope(qf[..., MLA_NOPE:], cos_r, sin_r)
    kv = (ckv @ w_ukv).reshape(B, S, MLA_HEADS, MLA_NOPE + MLA_V)
    k_rope = jnp.broadcast_to(apply_rope(kr[:, :, None, :], cos_r, sin_r), (B, S, MLA_HEADS, MLA_ROPE))
    q = jnp.concatenate([qf[..., :MLA_NOPE], q_rope], axis=-1)
    k = jnp.concatenate([kv[..., :MLA_NOPE], k_rope], axis=-1)
    v = kv[..., MLA_NOPE:]
    o = causal_attention_blocked(q, k, v, (MLA_NOPE + MLA_ROPE) ** -0.5)
    return o.reshape(B, S, MLA_HEADS * MLA_V) @ w_o


def dilated_group_attention(q, k, v, window, dilation):
    B, S, H, dh = q.shape
    scale = dh ** -0.5
    span = window // dilation
    unit = DIL_BLOCK * dilation
    sp = -(-S // unit) * unit
    L = sp // dilation
    nb = L // DIL_BLOCK
    padw = ((0, 0), (0, sp - S), (0, 0), (0, 0))

    def to_blocks(t):
        t = jnp.pad(t, padw).reshape(B, L, dilation, H, dh).transpose(0, 3, 2, 1, 4)
        return t.reshape(B, H, dilation, nb, DIL_BLOCK, dh)

    def with_prev(t):
        prev = jnp.pad(t, ((0, 0), (0, 0), (0, 0), (1, 0), (0, 0), (0, 0)))[:, :, :, :-1]
        return jnp.concatenate([prev, t], axis=4)

    qb = to_blocks(q)
    kk = with_prev(to_blocks(k))
    vv = with_prev(to_blocks(v))
    s = jnp.einsum("bhrnqd,bhrnkd->bhrnqk", qb, kk, preferred_element_type=jnp.float32) * scale
    qi = jnp.arange(DIL_BLOCK)[:, None]
    kj = jnp.arange(2 * DIL_BLOCK)[None, :]
    dist = qi + DIL_BLOCK - kj
    key_sub = jnp.arange(nb)[:, None, None] * DIL_BLOCK - DIL_BLOCK + kj[None]
    mask = (dist >= 0)[None] & (dist <= span)[None] & (key_sub >= 0)
    s = jnp.where(mask, s, NEG_INF)
    lse = jax.nn.logsumexp(s, axis=-1)
    p = jnp.exp(s - lse[..., None]).astype(v.dtype)
    o = jnp.einsum("bhrnqk,bhrnkd->bhrnqd", p, vv)
    o = o.reshape(B, H, dilation, L, dh).transpose(0, 3, 2, 1, 4).reshape(B, sp, H, dh)[:, :S]
    lse = lse.reshape(B, H, dilation, L).transpose(0, 3, 2, 1).reshape(B, sp, H)[:, :S]
    return o, lse


def dilated_mixer(xn, w_qkv, w_o, cos, sin):
    B, S, _ = xn.shape
    G = len(DIL_GROUPS)
    qkv = (xn @ w_qkv).reshape(B, S, G, 3, DIL_HEADS, HEAD_DIM)
    outs, lses = [], []
    for g, (window, dilation) in enumerate(DIL_GROUPS):
        q = apply_rope(qkv[:, :, g, 0], cos, sin)
        k = apply_rope(qkv[:, :, g, 1], cos, sin)
        o, lse = dilated_group_attention(q, k, qkv[:, :, g, 2], window, dilation)
        outs.append(o)
        lses.append(lse)
    wts = jax.nn.softmax(jnp.stack(lses, axis=0), axis=0)
    o = jnp.einsum("gbsh,gbshd->bshd", wts, jnp.stack(outs, axis=0).astype(jnp.float32))
    return o.reshape(B, S, DIL_HEADS * HEAD_DIM).astype(xn.dtype) @ w_o


def stick_breaking_attention(q, k, v):
    B, S, H, dh = q.shape
    scale = dh ** -0.5
    nq = S // SB_QBLOCK
    qb = q.reshape(B, nq, SB_QBLOCK, H, dh).transpose(1, 0, 2, 3, 4)
    key_pos = jnp.arange(S)

    def step(args):
        i, qi = args
        z = jnp.einsum("bqhd,bkhd->bhqk", qi, k, preferred_element_type=jnp.float32) * scale
        qpos = i * SB_QBLOCK + jnp.arange(SB_QBLOCK)
        past = key_pos[None, :] < qpos[:, None]
        log_beta = jnp.where(past, jax.nn.log_sigmoid(z), NEG_INF)
        log_stay = jnp.where(past, jax.nn.log_sigmoid(-z), 0.0)
        after = jnp.concatenate([log_stay[..., 1:], jnp.zeros_like(log_stay[..., :1])], axis=-1)
        stay = lax.cumsum(after, axis=3, reverse=True)
        a = jnp.exp(log_beta + stay).astype(v.dtype)
        return jnp.einsum("bhqk,bkhd->bqhd", a, v)

    out = lax.map(step, (jnp.arange(nq), qb))
    return out.transpose(1, 0, 2, 3, 4).reshape(B, S, H, dh)


def stick_breaking_mixer(xn, w_qkv, w_o):
    B, S, _ = xn.shape
    qkv = (xn @ w_qkv).reshape(B, S, 3, SB_HEADS, HEAD_DIM)
    o = stick_breaking_attention(qkv[:, :, 0], qkv[:, :, 1], qkv[:, :, 2])
    return o.reshape(B, S, SB_HEADS * HEAD_DIM) @ w_o


def conv_ffn(xn, w_gate, conv_w, conv_b, w_up, w_down):
    S = xn.shape[1]
    g = xn @ w_gate
    gp = jnp.pad(g, ((0, 0), (CONV_WIDTH - 1, 0), (0, 0)))
    gc = conv_b + sum(conv_w[j] * gp[:, j:j + S] for j in range(CONV_WIDTH))
    h = jax.nn.gelu(gc, approximate=False) * (xn @ w_up)
    return h @ w_down


def _n_layers_of(kind):
    return len(range(kind, DEPTH, N_MIXERS))


def setup_inputs(seed: int = 0) -> dict:
    key = jax.random.key(seed)
    ks = jax.random.split(key, 24)
    f32 = jnp.float32
    nA, nB, nC, nD = (_n_layers_of(m) for m in range(N_MIXERS))
    hd = N_HEADS * HEAD_DIM

    def dense(k, shape, fan_in):
        return jax.random.normal(k, shape, f32) * (fan_in ** -0.5)

    def gain(k, shape):
        return 1.0 + 0.02 * jax.random.normal(k, shape, f32)

    return {
        "x": jax.random.normal(ks[0], (BATCH, SEQ, D_MODEL), f32),
        "norm_mix": gain(ks[1], (DEPTH, D_MODEL)),
        "norm_ffn": gain(ks[2], (DEPTH, D_MODEL)),
        "norm_final": gain(ks[3], (D_MODEL,)),
        "a_w_qkv": dense(ks[4], (nA, D_MODEL, 3 * hd), D_MODEL),
        "a_w_o": dense(ks[5], (nA, hd, D_MODEL), hd),
        "b_w_in": dense(ks[6], (nB, D_MODEL, MLA_Q_LORA + MLA_KV_LORA + MLA_ROPE), D_MODEL),
        "b_q_norm": gain(ks[7], (nB, MLA_Q_LORA)),
        "b_w_uq": dense(ks[8], (nB, MLA_Q_LORA, MLA_HEADS * (MLA_NOPE + MLA_ROPE)), MLA_Q_LORA),
        "b_kv_norm": gain(ks[9], (nB, MLA_KV_LORA)),
        "b_w_ukv": dense(ks[10], (nB, MLA_KV_LORA, MLA_HEADS * (MLA_NOPE + MLA_V)), MLA_KV_LORA),
        "b_w_o": dense(ks[11], (nB, MLA_HEADS * MLA_V, D_MODEL), MLA_HEADS * MLA_V),
        "c_w_qkv": dense(ks[12], (nC, D_MODEL, len(DIL_GROUPS) * 3 * DIL_HEADS * HEAD_DIM), D_MODEL),
        "c_w_o": dense(ks[13], (nC, DIL_HEADS * HEAD_DIM, D_MODEL), DIL_HEADS * HEAD_DIM),
        "d_w_qkv": dense(ks[14], (nD, D_MODEL, 3 * SB_HEADS * HEAD_DIM), D_MODEL),
        "d_w_o": dense(ks[15], (nD, SB_HEADS * HEAD_DIM, D_MODEL), SB_HEADS * HEAD_DIM),
        "ffn_w_gate": dense(ks[16], (DEPTH, D_MODEL, D_FF), D_MODEL),
        "ffn_conv_w": dense(ks[17], (DEPTH, CONV_WIDTH, D_FF), CONV_WIDTH),
        "ffn_conv_b": 0.02 * jax.random.normal(ks[18], (DEPTH, D_FF), f32),
        "ffn_w_up": dense(ks[19], (DEPTH, D_MODEL, D_FF), D_MODEL),
        "ffn_w_down": dense(ks[20], (DEPTH, D_FF, D_MODEL), D_FF),
    }


def reference(x, norm_mix, norm_ffn, norm_final,
              a_w_qkv, a_w_o,
              b_w_in, b_q_norm, b_w_uq, b_kv_norm, b_w_ukv, b_w_o,
              c_w_qkv, c_w_o,
              d_w_qkv, d_w_o,
              ffn_w_gate, ffn_conv_w, ffn_conv_b, ffn_w_up, ffn_w_down):
    S = x.shape[1]
    pos = jnp.arange(S, dtype=jnp.float32)
    cos_h, sin_h = rope_tables(pos, HEAD_DIM)
    cos_r, sin_r = rope_tables(pos, MLA_ROPE)
    h = x
    for i in range(DEPTH):
        kind, j = i % N_MIXERS, i // N_MIXERS
        hn = rms_norm(h, norm_mix[i])
        if kind == 0:
            mix = moba_mixer(hn, a_w_qkv[j], a_w_o[j], cos_h, sin_h)
        elif kind == 1:
            mix = mla_mixer(hn, b_w_in[j], b_q_norm[j], b_w_uq[j], b_kv_norm[j], b_w_ukv[j], b_w_o[j], cos_r, sin_r)
        elif kind == 2:
            mix = dilated_mixer(hn, c_w_qkv[j], c_w_o[j], cos_h, sin_h)
        else:
            mix = stick_breaking_mixer(hn, d_w_qkv[j], d_w_o[j])
        h = h + mix.astype(h.dtype)
        f = conv_ffn(rms_norm(h, norm_ffn[i]), ffn_w_gate[i], ffn_conv_w[i], ffn_conv_b[i], ffn_w_up[i], ffn_w_down[i])
        h = h + f.astype(h.dtype)
    return rms_norm(h, norm_final)
```

```python
import numpy as np
import ml_dtypes
import concourse.bass as bass
import concourse.mybir as mybir
from concourse.bass_utils import run_bass_kernel_spmd

F32 = mybir.dt.float32
BF16 = mybir.dt.bfloat16
AF = mybir.ActivationFunctionType
ALU = mybir.AluOpType
AX = mybir.AxisListType
NPBF = ml_dtypes.bfloat16

EPOCH = 30000


class Counter:
    def __init__(self, prog, name, step):
        self.prog, self.name, self.step = prog, name, step
        self.n = 0
        self.sems = []
        self.per = EPOCH // step

    def _sem(self, e):
        while len(self.sems) <= e:
            self.sems.append(self.prog.nc.alloc_semaphore(name=f"{self.name}_{len(self.sems)}"))
        return self.sems[e]

    def next(self):
        self.n += 1
        return self._sem((self.n - 1) // self.per), self.n

    def wait_args(self, k):
        e = (k - 1) // self.per
        return self._sem(e), ((k - 1) % self.per + 1) * self.step


class Buf:
    __slots__ = ("name", "w", "r", "dmac")

    def __init__(self, name):
        self.name = name
        self.w = None
        self.r = {}
        self.dmac = None


class Prog:
    ENG = ("pe", "dve", "act", "pool", "sp")

    def __init__(self, nc):
        self.nc = nc
        self.q = {e: [] for e in self.ENG}
        self.cnt = {e: Counter(self, "c_" + e, 1) for e in self.ENG if e != "sp"}
        self.seen = {e: {} for e in self.ENG}
        self.nbuf = 0
        self.all_dma_events = {}

    def buf(self, name=None):
        self.nbuf += 1
        return Buf(name or f"b{self.nbuf}")

    def bufs(self, n, name="p"):
        return [self.buf(f"{name}{i}") for i in range(n)]

    def _need(self, eng, ev, waits):
        if ev is None:
            return
        c, k = ev
        if self.seen[eng].get(c, 0) >= k:
            return
        if eng == "pe" and c is self.cnt.get("pe"):
            return
        waits[c] = max(waits.get(c, 0), k)

    def _deps(self, eng, reads, writes):
        waits = {}
        for b in reads:
            self._need(eng, b.w, waits)
        for b in writes:
            self._need(eng, b.w, waits)
            for c, k in b.r.items():
                self._need(eng, (c, k), waits)
        for c, k in waits.items():
            self.seen[eng][c] = k
        return [c.wait_args(k) for c, k in waits.items()]

    def op(self, eng, fn, reads=(), writes=()):
        waits = self._deps(eng, reads, writes)
        c = self.cnt[eng]
        sem, k = c.next()
        self.q[eng].append((fn, waits, (sem, 1)))
        ev = (c, k)
        for b in reads:
            b.r[c] = k
        for b in writes:
            b.w = ev
            b.r = {}
        return ev

    def dma(self, fn, reads=(), writes=(), q="sp"):
        waits = self._deps(q, reads, writes)
        owner = writes[0] if writes else reads[0]
        if owner.dmac is None:
            owner.dmac = Counter(self, "d_" + owner.name, 16)
        c = owner.dmac
        sem, k = c.next()
        self.q[q].append((fn, waits, (sem, 16)))
        ev = (c, k)
        for b in reads:
            b.r[c] = k
        for b in writes:
            b.w = ev
            b.r = {}
        self.all_dma_events[c] = k
        return ev

    def finish(self):
        waits = {}
        for c, k in self.all_dma_events.items():
            self._need("sp", (c, k), waits)
        for e, c in self.cnt.items():
            if c.n:
                self._need("sp", (c, c.n), waits)
        wl = [c.wait_args(k) for c, k in waits.items()]
        self.q["sp"].append((None, wl, None))

    def emit(self):
        nc = self.nc
        handles = {"pe": "tensor", "dve": "vector", "act": "scalar", "pool": "gpsimd", "sp": "sync"}
        with nc.Block() as block:
            for e in self.ENG:
                ops = self.q[e]
                if not ops:
                    continue

                def body(eng, ops=ops):
                    for fn, waits, inc in ops:
                        for sem, val in waits:
                            eng.wait_ge(sem, val)
                        if fn is not None:
                            ins = fn(eng)
                            if inc is not None:
                                ins.then_inc(inc[0], inc[1])

                getattr(block, handles[e])(body)


D_MODEL = 1024
D_FF = 2816
NFC = D_FF // 128
RMS_EPS = 1e-6


def DMA(out, in_):
    return lambda e: e.dma_start(out=out, in_=in_)


def COPY(out, in_):
    return lambda e: e.tensor_copy(out, in_)


def MM(out, lhsT, rhs, start, stop):
    return lambda e: e.matmul(out, lhsT, rhs, start=start, stop=stop)


def TT(out, a, b, op):
    return lambda e: e.tensor_tensor(out, a, b, op)


def ACTF(out, in_, func, **kw):
    return lambda e: e.activation(out, in_, func, **kw)


def STT(out, in0, scalar, in1, op0, op1):
    return lambda e: e.scalar_tensor_tensor(out, in0, scalar, in1, op0, op1)


def TS(out, in0, s1, s2, op0, op1):
    return lambda e: e.tensor_scalar(out, in0, s1, s2, op0, op1)


class K:
    def __init__(self):
        self.nc = bass.Bass("TRN2", target_bir_lowering=False)
        self.p = Prog(self.nc)
        self.rot = {}

    def din(self, name, shape, dt):
        return self.nc.dram_tensor(name, list(shape), dt, kind="ExternalInput").ap()

    def dout(self, name, shape, dt):
        return self.nc.dram_tensor(name, list(shape), dt, kind="ExternalOutput").ap()

    def sb(self, name, shape, dt):
        return self.nc.alloc_sbuf_tensor("s_" + name, list(shape), dt), self.p.buf(name)

    def sbn(self, name, n, shape, dt):
        return [self.sb(f"{name}{i}", shape, dt) for i in range(n)]

    def banks(self):
        out = []
        for i in range(8):
            t = self.nc.alloc_psum_tensor(f"bank{i}", [128, 512], F32)
            out.append((t, self.p.buf(f"bank{i}")))
        return out

    def nxt(self, key, lst):
        i = self.rot.get(key, 0)
        self.rot[key] = i + 1
        return lst[i % len(lst)]

    def load_const(self, ap_d, shape, dt, name):
        t, B = self.sb(name, shape, dt)
        self.p.dma(DMA(t[:], ap_d), writes=[B])
        return t, B

    def load_w(self, w_ap, krows, ncols, dst, DST, stgs, col0=0):
        i = 0
        for kc in range(krows // 128):
            for c0 in range(0, ncols, 512):
                cw = min(512, ncols - c0)
                stg, STG = self.nxt("stg", stgs)
                self.p.dma(DMA(stg[:, 0:cw], w_ap[kc * 128:(kc + 1) * 128, c0:c0 + cw]), writes=[STG])
                eng = "pool" if i % 2 == 0 else "dve"
                i += 1
                self.p.op(eng, COPY(dst[:, kc, col0 + c0:col0 + c0 + cw], stg[:, 0:cw]), reads=[STG], writes=[DST])


def rms_front(k, h, HB, ss, rr, SSB, col, g_rep, GB, hn, HNB, junk, JB, width=D_MODEL, eps=RMS_EPS):
    p = k.p
    p.op("act", ACTF(junk[:, 0:width], h[:, 0:width], AF.Square, accum_out=ss[:, col:col + 1]),
         reads=[HB], writes=[JB, SSB])
    p.op("act", ACTF(rr[:, col:col + 1], ss[:, col:col + 1], AF.Sqrt, bias=eps, scale=1.0 / width),
         reads=[SSB], writes=[SSB])
    p.op("dve", lambda e: e.reciprocal(rr[:, col:col + 1], rr[:, col:col + 1]), reads=[SSB], writes=[SSB])
    p.op("dve", STT(hn[:, 0:width], h[:, 0:width], rr[:, col:col + 1], g_rep[:, 0:width], ALU.mult, ALU.mult),
         reads=[HB, SSB, GB], writes=[HNB])


def transpose_to(k, src, SRCB, nblk, ident, IDB, bank, BANKB, dst_view, DSTB, eng="act"):
    p = k.p
    bv = bank[:].bitcast(BF16)
    for c in range(nblk):
        p.op("pe", (lambda o, i: (lambda e: e.transpose(o, i, ident[:])))(bv[:, c * 128:(c + 1) * 128],
                                                                          src[:, c * 128:(c + 1) * 128]),
             reads=[SRCB, IDB], writes=[BANKB])
    src_v = bv[:, 0:nblk * 128].rearrange("p (c t) -> p c t", t=128)
    p.op(eng, COPY_ACT(dst_view, src_v) if eng == "act" else COPY(dst_view, src_v), reads=[BANKB], writes=[DSTB])


def build_F(S2, last):
    k = K()
    p = k.p
    D = D_MODEL
    NT = S2 // 512
    h_in = k.din("h_in", [128 + S2, D], F32)
    ot_in = k.din("ot_in", [D, 128 + S2], BF16)
    w_o = k.din("w_o", [D, D], F32)
    w_g = k.din("w_gate", [D, D_FF], F32)
    w_u = k.din("w_up", [D, D_FF], F32)
    w_d = k.din("w_down", [D_FF, D], F32)
    g_ffn = k.din("g_ffn", [1, D], F32)
    cwb_d = k.din("cwb", [128, NFC * 4], F32)
    ident_d = k.din("ident", [128, 128], BF16)
    if last:
        g_fin = k.din("g_fin", [1, D], F32)
    h_out = k.dout("h_out", [S2, D], F32)

    banks = k.banks()
    poolA, poolB = banks[0:4], banks[4:8]
    ident, IDB = k.load_const(ident_d, [128, 128], BF16, "ident")
    g_rep, GB = k.load_const(g_ffn.partition_broadcast(128), [128, D], F32, "g_rep")
    if last:
        gf_rep, GFB = k.load_const(g_fin.partition_broadcast(128), [128, D], F32, "gf_rep")
    cwb, CWB = k.load_const(cwb_d, [128, NFC * 4], F32, "cwb")
    stgs = k.sbn("stg", 4, [128, 512], F32)
    wo, WOB = k.sb("wo", [128, 8, D], BF16)
    wg, WGB = k.sb("wg", [128, 8, D_FF], BF16)
    wu, WUB = k.sb("wu", [128, 8, D_FF], BF16)
    k.load_w(w_o, D, D, wo, WOB, stgs)
    k.load_w(w_g, D, D_FF, wg, WGB, stgs)
    k.load_w(w_u, D, D_FF, wu, WUB, stgs)
    wdbs = k.sbn("wdb", 4, [128, 512], BF16)
    hs = k.sbn("h", 5, [128, D], F32)
    ott, OTB = k.sb("ott", [128, 8, 512], BF16)
    hns = k.sbn("hn", 2, [128, D], BF16)
    hnT, HNTB = k.sb("hnT", [128, 8, 512], BF16)
    h1T, H1B = k.sb("h1T", [128, NFC, 512], BF16)
    gxs = k.sbn("gx", 2, [128, 514], F32)
    tts = k.sbn("tt", 2, [128, 512], F32)
    ges = k.sbn("ge", 2, [128, 512], F32)
    junk, JB = k.sb("junk", [128, D], BF16)
    halo, HALOB = k.sb("halo", [128, NFC, 2], F32)
    nstat = 2 * (NT * 4 + 1) + 2
    ss, SSB = k.sb("ss", [128, nstat], F32)
    rr, _ = k.sb("rr", [128, nstat], F32)
    p.op("dve", lambda e: e.memset(ss[:], 0.0), writes=[SSB])
    statc = [0]

    def tile(t_in0, ntok, is_halo, out0):
        nst = ntok // 128
        p.dma(DMA(ott[:, :, 0:ntok], ot_in.rearrange("(kc q) t -> q kc t", q=128)[:, :, t_in0:t_in0 + ntok]),
              writes=[OTB])
        hsub = []
        for st in range(nst):
            h, HB = k.nxt("h", hs)
            hsub.append((h, HB))
            p.dma(DMA(h[:], h_in[t_in0 + st * 128:t_in0 + (st + 1) * 128, :]), writes=[HB])
            for dh in range(2):
                bank, BB = k.nxt("pa", poolA)
                for kc in range(8):
                    p.op("pe", MM(bank[:, :], ott[:, kc, st * 128:(st + 1) * 128], wo[:, kc, dh * 512:(dh + 1) * 512],
                                  kc == 0, kc == 7), reads=[OTB, WOB], writes=[BB])
                p.op("dve", TT(h[:, dh * 512:(dh + 1) * 512], bank[:, :], h[:, dh * 512:(dh + 1) * 512], ALU.add),
                     reads=[BB, HB], writes=[HB])
            hn, HNB = k.nxt("hn", hns)
            col = statc[0]
            statc[0] += 1
            rms_front(k, h, HB, ss, rr, SSB, col, g_rep, GB, hn, HNB, junk, JB)
            bank, BB = k.nxt("pa", poolA)
            transpose_to(k, hn, HNB, 8, ident, IDB, bank, BB, hnT[:, :, st * 128:(st + 1) * 128], HNTB)
        for fc in range(NFC):
            G, GBk = k.nxt("pa", poolA)
            for kc in range(8):
                p.op("pe", MM(G[:, 0:ntok], wg[:, kc, fc * 128:(fc + 1) * 128], hnT[:, kc, 0:ntok], kc == 0, kc == 7),
                     reads=[WGB, HNTB], writes=[GBk])
            gx, GXB = k.nxt("gx", gxs)
            p.op("act", COPY_ACT(gx[:, 2:2 + ntok], G[:, 0:ntok]), reads=[GBk], writes=[GXB])
            if is_halo:
                p.op("pool", COPY(halo[:, fc, :], gx[:, ntok:ntok + 2]), reads=[GXB], writes=[HALOB])
                continue
            U, UBk = k.nxt("pa", poolA)
            for kc in range(8):
                p.op("pe", MM(U[:, 0:ntok], wu[:, kc, fc * 128:(fc + 1) * 128], hnT[:, kc, 0:ntok], kc == 0, kc == 7),
                     reads=[WUB, HNTB], writes=[UBk])
            p.op("pool", COPY(gx[:, 0:2], halo[:, fc, :]), reads=[HALOB], writes=[GXB])
            p.op("pool", COPY(halo[:, fc, :], gx[:, 512:514]), reads=[GXB], writes=[HALOB])
            tt, TTB = k.nxt("tt", tts)
            c4 = fc * 4
            p.op("act", ACTF(tt[:, :], G[:, :], AF.Identity, scale=cwb[:, c4 + 2:c4 + 3], bias=cwb[:, c4 + 3:c4 + 4]),
                 reads=[GBk, CWB], writes=[TTB])
            p.op("dve", STT(tt[:, :], gx[:, 1:513], cwb[:, c4 + 1:c4 + 2], tt[:, :], ALU.mult, ALU.add),
                 reads=[GXB, CWB, TTB], writes=[TTB])
            p.op("dve", STT(tt[:, :], gx[:, 0:512], cwb[:, c4:c4 + 1], tt[:, :], ALU.mult, ALU.add),
                 reads=[GXB, CWB, TTB], writes=[TTB])
            ge, GEB = k.nxt("ge", ges)
            p.op("act", ACTF(ge[:, :], tt[:, :], AF.Gelu), reads=[TTB], writes=[GEB])
            p.op("dve", TT(h1T[:, fc, :], U[:, :], ge[:, :], ALU.mult), reads=[UBk, GEB], writes=[H1B])
        if is_halo:
            return
        for dh in range(2):
            for fc in range(NFC):
                stg, STG = k.nxt("stg", stgs)
                p.dma(DMA(stg[:, :], w_d[fc * 128:(fc + 1) * 128, dh * 512:(dh + 1) * 512]), writes=[STG])
                wdb, WDB = k.nxt("wdb", wdbs)
                p.op("pool", COPY(wdb[:, :], stg[:, :]), reads=[STG], writes=[WDB])
                for st in range(4):
                    bank, BB = poolB[st]
                    p.op("pe", MM(bank[:, :], h1T[:, fc, st * 128:(st + 1) * 128], wdb[:, :], fc == 0, fc == NFC - 1),
                         reads=[H1B, WDB], writes=[BB])
            for st in range(4):
                bank, BB = poolB[st]
                h, HB = hsub[st]
                p.op("dve", TT(h[:, dh * 512:(dh + 1) * 512], bank[:, :], h[:, dh * 512:(dh + 1) * 512], ALU.add),
                     reads=[BB, HB], writes=[HB])
        for st in range(4):
            h, HB = hsub[st]
            if last:
                col = statc[0]
                statc[0] += 1
                p.op("act", ACTF(junk[:, :], h[:, :], AF.Square, accum_out=ss[:, col:col + 1]), reads=[HB], writes=[JB, SSB])
                p.op("act", ACTF(rr[:, col:col + 1], ss[:, col:col + 1], AF.Sqrt, bias=RMS_EPS, scale=1.0 / D),
                     reads=[SSB], writes=[SSB])
                p.op("dve", lambda e, col=col: e.reciprocal(rr[:, col:col + 1], rr[:, col:col + 1]), reads=[SSB], writes=[SSB])
                p.op("dve", STT(h[:, :], h[:, :], rr[:, col:col + 1], gf_rep[:, :], ALU.mult, ALU.mult),
                     reads=[HB, SSB, GFB], writes=[HB])
                p.dma(DMA(h_out[out0 + st * 128:out0 + (st + 1) * 128, :], h[:, :]), reads=[HB])
            else:
                p.dma(DMA(h_out[out0 + st * 128:out0 + (st + 1) * 128, :], h[:, :]), reads=[HB])

    tile(0, 128, True, 0)
    for t in range(NT):
        tile(128 + t * 512, 512, False, t * 512)
    p.finish()
    p.emit()
    return k.nc


def COPY_ACT(out, in_):
    return lambda e: e.copy(out, in_)


def build_P(S2, rope):
    k = K()
    p = k.p
    D = D_MODEL
    NT = S2 // 512
    h_in = k.din("h_in", [S2, D], F32)
    g_mix = k.din("g_mix", [1, D], F32)
    ident_d = k.din("ident", [128, 128], BF16)
    wd = {n: k.din(n, [D, D], F32) for n in (("w_q", "w_k", "w_v", "w_qs", "w_ks") if rope else ("w_q", "w_k", "w_v"))}
    if rope:
        cos_d = k.din("cosT", [128, S2], F32)
        sin_d = k.din("sinT", [128, S2], F32)
    qt_o = k.dout("qt", [D, S2], BF16)
    kt_o = k.dout("kt", [D, S2], BF16)
    v_o = k.dout("v", [S2, D], BF16)

    banks = k.banks()
    ident, IDB = k.load_const(ident_d, [128, 128], BF16, "ident")
    g_rep, GB = k.load_const(g_mix.partition_broadcast(128), [128, D], F32, "g_rep")
    stgs = k.sbn("stg", 4, [128, 512], F32)
    W = {}
    for n, ap in wd.items():
        t, B = k.sb(n, [128, 8, D], BF16)
        k.load_w(ap, D, D, t, B, stgs)
        W[n] = (t, B)
    hs = k.sbn("h", 3, [128, D], F32)
    hns = k.sbn("hn", 2, [128, D], BF16)
    hnTs = k.sbn("hnT", 2, [128, 8, 512], BF16)
    junk, JB = k.sb("junk", [128, D], BF16)
    ss, SSB = k.sb("ss", [128, NT * 4], F32)
    rr, _ = k.sb("rr", [128, NT * 4], F32)
    p.op("dve", lambda e: e.memset(ss[:], 0.0), writes=[SSB])
    outs = k.sbn("ost", 2, [128, 8, 512], BF16)
    vsts = k.sbn("vst", 2, [128, D], BF16)
    if rope:
        coss = k.sbn("cos", 2, [128, 512], F32)
        sins = k.sbn("sin", 2, [128, 512], F32)
        t1s = k.sbn("t1", 2, [128, 512], F32)
        t2s = k.sbn("t2", 2, [128, 512], F32)
    for t in range(NT):
        hnT, HNTB = k.nxt("hnT", hnTs)
        for st in range(4):
            h, HB = k.nxt("h", hs)
            r0 = t * 512 + st * 128
            p.dma(DMA(h[:], h_in[r0:r0 + 128, :]), writes=[HB])
            hn, HNB = k.nxt("hn", hns)
            rms_front(k, h, HB, ss, rr, SSB, t * 4 + st, g_rep, GB, hn, HNB, junk, JB)
            bank, BB = k.nxt("pa", banks)
            transpose_to(k, hn, HNB, 8, ident, IDB, bank, BB, hnT[:, :, st * 128:(st + 1) * 128], HNTB)
        if rope:
            cs, CSB = k.nxt("cos", coss)
            sn, SNB = k.nxt("sin", sins)
            p.dma(DMA(cs[:], cos_d[:, t * 512:(t + 1) * 512]), writes=[CSB])
            p.dma(DMA(sn[:], sin_d[:, t * 512:(t + 1) * 512]), writes=[SNB])
        for which, dst in (("q", qt_o), ("k", kt_o)):
            w, WB = W["w_" + which]
            ost, OSB = k.nxt("ost", outs)
            for hp in range(8):
                bank, BB = k.nxt("pa", banks)
                for kc in range(8):
                    p.op("pe", MM(bank[:, :], w[:, kc, hp * 128:(hp + 1) * 128], hnT[:, kc, :], kc == 0, kc == 7),
                         reads=[WB, HNTB], writes=[BB])
                if rope:
                    w2, W2B = W["w_" + which + "s"]
                    bank2, BB2 = k.nxt("pa", banks)
                    for kc in range(8):
                        p.op("pe", MM(bank2[:, :], w2[:, kc, hp * 128:(hp + 1) * 128], hnT[:, kc, :], kc == 0, kc == 7),
                             reads=[W2B, HNTB], writes=[BB2])
                    t1, T1B = k.nxt("t1", t1s)
                    t2, T2B = k.nxt("t2", t2s)
                    p.op("dve", TT(t1[:, :], bank[:, :], cs[:, :], ALU.mult), reads=[BB, CSB], writes=[T1B])
                    p.op("dve", TT(t2[:, :], bank2[:, :], sn[:, :], ALU.mult), reads=[BB2, SNB], writes=[T2B])
                    p.op("pool", TT(ost[:, hp, :], t1[:, :], t2[:, :], ALU.add), reads=[T1B, T2B], writes=[OSB])
                else:
                    p.op("act", COPY_ACT(ost[:, hp, :], bank[:, :]), reads=[BB], writes=[OSB])
            p.dma(DMA(dst.rearrange("(hp q) t -> q hp t", q=128)[:, :, t * 512:(t + 1) * 512], ost[:, :, :]), reads=[OSB])
        wv, WVB = W["w_v"]
        for st in range(4):
            vst, VSB = k.nxt("vst", vsts)
            for cg in range(2):
                bank, BB = k.nxt("pa", banks)
                for kc in range(8):
                    p.op("pe", MM(bank[:, :], hnT[:, kc, st * 128:(st + 1) * 128], wv[:, kc, cg * 512:(cg + 1) * 512],
                                  kc == 0, kc == 7), reads=[WVB, HNTB], writes=[BB])
                p.op("act", COPY_ACT(vst[:, cg * 512:(cg + 1) * 512], bank[:, :]), reads=[BB], writes=[VSB])
            r0 = t * 512 + st * 128
            p.dma(DMA(v_o[r0:r0 + 128, :], vst[:, :]), reads=[VSB])
    p.finish()
    p.emit()
    return k.nc


def attn_common(k, S, dk, NH):
    p = k.p
    NB = S // 128
    c = {}
    c["qt_d"] = k.din("qt", [NH * dk, S], BF16)
    c["kt_d"] = k.din("kt", [NH * dk, S], BF16)
    c["v_d"] = k.din("v", [NH, 128, NB * 64], BF16)
    c["banks"] = k.banks()
    c["tri"], c["TRIB"] = k.load_const(k.din("tri", [128, 128], BF16), [128, 128], BF16, "tri")
    c["kts"] = k.sbn("ktb", 2, [128, S], BF16)
    c["qts"] = k.sbn("qtb", 2, [128, S], BF16)
    c["vas"] = k.sbn("vab", 2, [128, NB, 128], BF16)
    for va, VAB in c["vas"]:
        p.op("pool", (lambda va: lambda e: e.memset(va[:, :, 64:128], 1.0))(va), writes=[VAB])
    zt, ZB = k.sb("zt", [128, 512], BF16)
    p.op("pool", lambda e: e.memset(zt[:], 0.0), writes=[ZB])
    c["zt"], c["ZB"] = zt, ZB
    c["pts"] = k.sbn("pt", 3, [128, 512], BF16)
    return c


def load_head(k, c, idx, hrow, dk, S):
    p = k.p
    KT, KTB = c["kts"][idx % 2]
    QT, QTB = c["qts"][idx % 2]
    VA, VAB = c["vas"][idx % 2]
    p.dma(DMA(KT[0:dk, :], c["kt_d"][hrow * dk:(hrow + 1) * dk, :]), writes=[KTB])
    p.dma(DMA(QT[0:dk, :], c["qt_d"][hrow * dk:(hrow + 1) * dk, :]), writes=[QTB])
    p.dma(DMA(VA[:, :, 0:64], c["v_d"][hrow].rearrange("q (n d) -> q n d", d=64)), writes=[VAB])
    return KT, KTB, QT, QTB, VA, VAB


def build_A(S, dk, moba):
    k = K()
    p = k.p
    NH = 8
    NB = S // 128
    NQT = S // 512
    scale = float(dk) ** -0.5
    c = attn_common(k, S, dk, NH)
    ot_o = k.dout("ot", [NH * 64, S], BF16)
    banks = c["banks"]
    sc_banks, o_banks, misc = banks[0:3], banks[3:5], banks[5:8]
    tri, TRIB = c["tri"], c["TRIB"]
    zt, ZB = c["zt"], c["ZB"]
    ots = k.sbn("otb", 2, [64, S], BF16)
    rdens = k.sbn("rd", 2, [128, 512], F32)
    if moba:
        NBLK = S // 256
        ident, IDB = k.load_const(k.din("ident", [128, 128], BF16), [128, 128], BF16, "ident")
        ee, EEB = k.load_const(k.din("ee", [NBLK, NBLK * 128], BF16), [NBLK, NBLK * 128], BF16, "ee")
        mts = k.sbn("mt", 2, [NBLK, S], BF16)
        for mt, MTB in mts:
            p.op("pool", (lambda mt: lambda e: e.memset(mt[:, 0:256], 0.0))(mt), writes=[MTB])
        ks, KSB = k.sb("ks", [64, NBLK], F32)
        kshi, KHB = k.sb("kshi", [64, NBLK], BF16)
        kslo, KLB = k.sb("kslo", [64, NBLK], BF16)
        gms = k.sbn("gm", 2, [128, NBLK], F32)
        mxs = k.sbn("mx", 2, [128, 8], F32)
        mms = k.sbn("mm", 2, [128, NBLK], BF16)
    for h in range(NH):
        KT, KTB, QT, QTB, VA, VAB = load_head(k, c, h, h, dk, S)
        if moba:
            MT, MTB = mts[h % 2]
            p.op("dve", lambda e, KT=KT: e.tensor_reduce(out=ks[:, :], in_=KT[0:64, :].rearrange("q (n j) -> q n j", j=256),
                                                        axis=AX.X, op=ALU.add), reads=[KTB], writes=[KSB])
            p.op("dve", COPY(kshi[:, :], ks[:, :]), reads=[KSB], writes=[KHB])
            p.op("dve", TT(kslo[:, :], ks[:, :], kshi[:, :], ALU.subtract), reads=[KSB, KHB], writes=[KLB])
            for qb in range(2, NB):
                blk = qb // 2
                gb, GBB = k.nxt("misc", misc)
                p.op("pe", MM(gb[:, 0:NBLK], QT[0:64, qb * 128:(qb + 1) * 128], kshi[:, :], True, False),
                     reads=[QTB, KHB], writes=[GBB])
                p.op("pe", MM(gb[:, 0:NBLK], QT[0:64, qb * 128:(qb + 1) * 128], kslo[:, :], False, True),
                     reads=[QTB, KLB], writes=[GBB])
                gm, GMB = k.nxt("gm", gms)
                p.op("pool", (lambda gm: lambda e: e.memset(gm[:, :], -1e30))(gm), writes=[GMB])
                p.op("act", COPY_ACT(gm[:, 0:blk], gb[:, 0:blk]), reads=[GBB], writes=[GMB])
                mx, MXB = k.nxt("mx", mxs)
                p.op("dve", (lambda mx, gm: lambda e: e.max(out=mx[:, :], in_=gm[:, :]))(mx, gm), reads=[GMB], writes=[MXB])
                mm, MMB = k.nxt("mm", mms)
                p.op("dve", TS(mm[:, :], gm[:, :], mx[:, 2:3], -30000.0, ALU.is_lt, ALU.mult), reads=[GMB, MXB], writes=[MMB])
                p.op("pool", (lambda mm, blk: lambda e: e.memset(mm[:, blk:blk + 1], 0.0))(mm, blk), writes=[MMB])
                tb, TBB = k.nxt("misc", misc)
                tbv = tb[:].bitcast(BF16)
                p.op("pe", (lambda o, i: lambda e: e.transpose(o, i, ident[:]))(tbv[0:NBLK, 0:128], mm[:, :]),
                     reads=[MMB, IDB], writes=[TBB])
                p.op("act", COPY_ACT(MT[:, qb * 128:(qb + 1) * 128], tbv[0:NBLK, 0:128]), reads=[TBB], writes=[MTB])
        OT, OTB = ots[h % 2]
        for qt in range(NQT):
            q0 = qt * 512
            OA, OAB = k.nxt("oa", o_banks)
            p.op("pe", MM(OA[:, :], zt[:, 0:128], zt[:, :], True, False), reads=[ZB], writes=[OAB])
            steps = [(4 * qt + j, j * 128, True) for j in range(4)] + [(kb, 0, False) for kb in range(4 * qt)]

            def qk(i):
                kb, c0, diag = steps[i]
                n = 512 - c0
                SC, SCB = k.nxt("sc", sc_banks)
                masked = moba and (kb // 2) < 2 * qt + 1 and qt * 2 + 1 > 0 and not (qt == 0 and False)
                p.op("pe", MM(SC[:, 0:n], KT[0:dk, kb * 128:(kb + 1) * 128], QT[0:dk, q0 + c0:q0 + 512], True, not masked),
                     reads=[KTB, QTB], writes=[SCB])
                if masked:
                    nb_ = kb // 2
                    p.op("pe", MM(SC[:, 0:n], ee[:, nb_ * 128:(nb_ + 1) * 128], MT[:, q0 + c0:q0 + 512], False, True),
                         reads=[EEB, MTB], writes=[SCB])
                return SC, SCB

            cur = qk(0)
            for i in range(len(steps)):
                kb, c0, diag = steps[i]
                n = 512 - c0
                SC, SCB = cur
                if i + 1 < len(steps):
                    cur = qk(i + 1)
                PT, PTB = k.nxt("pt", c["pts"])
                p.op("act", ACTF(PT[:, 0:n], SC[:, 0:n], AF.Exp, scale=scale), reads=[SCB], writes=[PTB])
                if diag:
                    p.op("pool", TT(PT[:, 0:128], PT[:, 0:128], tri[:, :], ALU.mult), reads=[PTB, TRIB], writes=[PTB])
                p.op("pe", MM(OA[:, c0:512], VA[:, kb, :], PT[:, 0:n], False, i == len(steps) - 1),
                     reads=[VAB, PTB], writes=[OAB])
            rd, RDB = k.nxt("rd", rdens)
            p.op("dve", (lambda rd, OA: lambda e: e.reciprocal(rd[64:128, :], OA[64:128, :]))(rd, OA), reads=[OAB], writes=[RDB])
            p.op("dve", TT(OT[0:64, q0:q0 + 512], OA[0:64, :], rd[64:128, :], ALU.mult), reads=[OAB, RDB], writes=[OTB])
        p.dma(DMA(ot_o[h * 64:(h + 1) * 64, :], OT[0:64, :]), reads=[OTB])
    p.finish()
    p.emit()
    return k.nc


def build_SB(S, window=None):
    k = K()
    p = k.p
    NH, dk = 8, 64
    NQT = S // 512
    scale = 0.125
    c = attn_common(k, S, dk, NH)
    ot_o = k.dout("ot", [NH * 64, S], BF16)
    banks = c["banks"]
    z_banks, st_banks, o_banks = banks[0:2], banks[2:4], banks[4:6]
    zt, ZB = c["zt"], c["ZB"]
    tris, TRSB = k.load_const(k.din("tris", [128, 128], BF16), [128, 128], BF16, "tris")
    triu, TRUB = k.load_const(k.din("triu", [128, 128], BF16), [128, 128], BF16, "triu")
    ones, ONB = k.sb("ones", [128, 128], BF16)
    p.op("pool", lambda e: e.memset(ones[:], 1.0), writes=[ONB])
    ots = k.sbn("otb", 2, [64, S], BF16)
    es = k.sbn("e", 2, [128, 512], F32)
    ls = k.sbn("l", 2, [128, 512], F32)
    lss = k.sbn("ls", 2, [128, 512], BF16)
    args = k.sbn("arg", 2, [128, 512], F32)
    lsaccs = k.sbn("lsacc", 2, [128, 512], BF16)
    for h in range(NH):
        KT, KTB, QT, QTB, VA, VAB = load_head(k, c, h, h, dk, S)
        OT, OTB = ots[h % 2]
        for qt in range(NQT):
            q0 = qt * 512
            OA, OAB = k.nxt("oa", o_banks)
            p.op("pe", MM(OA[:, :], zt[:, 0:128], zt[:, :], True, False), reads=[ZB], writes=[OAB])
            LA, LAB = k.nxt("lsacc", lsaccs)
            p.op("pool", (lambda LA: lambda e: e.memset(LA[:, :], 0.0))(LA), writes=[LAB])
            lo = 0 if window is None else max(0, 4 * qt - window)
            steps = [(4 * qt + j, j * 128, True) for j in (3, 2, 1, 0)] + [(kb, 0, False) for kb in range(4 * qt - 1, lo - 1, -1)]

            def stageA(i):
                kb, c0, diag = steps[i]
                n = 512 - c0
                Z, ZBk = k.nxt("z", z_banks)
                p.op("pe", MM(Z[:, 0:n], KT[0:dk, kb * 128:(kb + 1) * 128], QT[0:dk, q0 + c0:q0 + 512], True, True),
                     reads=[KTB, QTB], writes=[ZBk])
                e_, EB = k.nxt("e", es)
                p.op("act", ACTF(e_[:, 0:n], Z[:, 0:n], AF.Exp, scale=-scale), reads=[ZBk], writes=[EB])
                l_, LB = k.nxt("l", ls)
                p.op("act", ACTF(l_[:, 0:n], e_[:, 0:n], AF.Ln, bias=1.0), reads=[EB], writes=[LB])
                LS, LSB = k.nxt("ls", lss)
                p.op("dve", STT(LS[:, 0:n], Z[:, 0:n], -scale, l_[:, 0:n], ALU.mult, ALU.subtract), reads=[ZBk, LB], writes=[LSB])
                if diag:
                    p.op("pool", TT(LS[:, 0:128], LS[:, 0:128], tris[:, :], ALU.mult), reads=[LSB, TRSB], writes=[LSB])
                return l_, LB, LS, LSB

            cur = stageA(0)
            for i in range(len(steps)):
                kb, c0, diag = steps[i]
                n = 512 - c0
                l_, LB, LS, LSB = cur
                if i + 1 < len(steps):
                    cur = stageA(i + 1)
                ST, STB = k.nxt("st", st_banks)
                if i > 0:
                    p.op("pe", MM(ST[:, 0:n], ones[:, :], LA[:, c0:512], True, False), reads=[ONB, LAB], writes=[STB])
                p.op("pe", MM(ST[:, 0:n], triu[:, :], LS[:, 0:n], i == 0, True), reads=[TRUB, LSB], writes=[STB])
                ar, ARB = k.nxt("arg", args)
                p.op("dve", TT(ar[:, 0:n], ST[:, 0:n], l_[:, 0:n], ALU.subtract), reads=[STB, LB], writes=[ARB])
                PT, PTB = k.nxt("pt", c["pts"])
                p.op("act", ACTF(PT[:, 0:n], ar[:, 0:n], AF.Exp), reads=[ARB], writes=[PTB])
                if diag:
                    p.op("pool", TT(PT[:, 0:128], PT[:, 0:128], tris[:, :], ALU.mult), reads=[PTB, TRSB], writes=[PTB])
                if i + 1 < len(steps):
                    p.op("pool", TT(LA[:, c0:512], LA[:, c0:512], LS[:, 0:n], ALU.add), reads=[LAB, LSB], writes=[LAB])
                p.op("pe", MM(OA[:, c0:512], VA[:, kb, 0:64], PT[:, 0:n], False, i == len(steps) - 1)
                     if False else MM(OA[0:64, c0:512], VA[:, kb, 0:64], PT[:, 0:n], False, i == len(steps) - 1),
                     reads=[VAB, PTB], writes=[OAB])
            p.op("act", COPY_ACT(OT[0:64, q0:q0 + 512], OA[0:64, :]), reads=[OAB], writes=[OTB])
        p.dma(DMA(ot_o[h * 64:(h + 1) * 64, :], OT[0:64, :]), reads=[OTB])
    p.finish()
    p.emit()
    return k.nc


DIL_GROUPS = ((128, 1), (512, 4), (2048, 16))


def build_DIL(S):
    k = K()
    p = k.p
    NHG = 24
    NB = S // 128
    scale = 0.125
    c = attn_common(k, S, 64, NHG)
    ot_o = k.dout("ot", [8 * 64, S], BF16)
    banks = c["banks"]
    sc_banks = banks[0:3]
    oregs = []
    for bi in (3, 4):
        for r in range(4):
            oregs.append((banks[bi][0][:, r * 128:(r + 1) * 128], p.buf(f"oreg{bi}_{r}")))
    band, BNB = k.load_const(k.din("band", [128, 256], BF16), [128, 256], BF16, "band")
    accs = k.sbn("acc", 1, [128, S], F32)
    den0, DNB = k.sb("den0", [64, S], F32)
    ots = k.sbn("otb", 1, [64, S], BF16)
    idx = 0
    for h in range(8):
        ACC, ACCB = accs[0]
        for g, (window, d) in enumerate(DIL_GROUPS):
            nbs = (S // d) // 128
            KT, KTB, QT, QTB, VA, VAB = load_head(k, c, idx, g * 8 + h, 64, S)
            idx += 1
            for nb in range(NB):
                r, n = divmod(nb, nbs)
                hp = n > 0
                SC, SCB = k.nxt("sc", sc_banks)
                qs = QT[0:64, nb * 128:(nb + 1) * 128]
                if hp:
                    p.op("pe", MM(SC[:, 0:128], KT[0:64, (nb - 1) * 128:nb * 128], qs, True, True), reads=[KTB, QTB], writes=[SCB])
                p.op("pe", MM(SC[:, 128:256], KT[0:64, nb * 128:(nb + 1) * 128], qs, True, True), reads=[KTB, QTB], writes=[SCB])
                lo = 0 if hp else 128
                PT, PTB = k.nxt("pt", c["pts"])
                p.op("act", ACTF(PT[:, lo:256], SC[:, lo:256], AF.Exp, scale=scale), reads=[SCB], writes=[PTB])
                p.op("pool", TT(PT[:, lo:256], PT[:, lo:256], band[:, lo:256], ALU.mult), reads=[PTB, BNB], writes=[PTB])
                OA, OAB = k.nxt("oreg", oregs)
                if hp:
                    p.op("pe", MM(OA, VA[:, nb - 1, :], PT[:, 0:128], True, False), reads=[VAB, PTB], writes=[OAB])
                p.op("pe", MM(OA, VA[:, nb, :], PT[:, 128:256], not hp, True), reads=[VAB, PTB], writes=[OAB])
                s0 = r + n * 128 * d
                dst = ACC[:, s0:s0 + 127 * d + 1:d] if d > 1 else ACC[:, s0:s0 + 128]
                if g == 0:
                    p.op("act", COPY_ACT(dst, OA), reads=[OAB], writes=[ACCB])
                else:
                    p.op("dve", TT(dst, OA, dst, ALU.add), reads=[OAB, ACCB], writes=[ACCB])
        OT, OTB = ots[0]
        p.op("dve", COPY(den0[0:64, :], ACC[64:128, :]), reads=[ACCB], writes=[DNB])
        p.op("dve", lambda e: e.reciprocal(den0[0:64, :], den0[0:64, :]), reads=[DNB], writes=[DNB])
        p.op("dve", TT(OT[0:64, :], ACC[0:64, :], den0[0:64, :], ALU.mult), reads=[ACCB, DNB], writes=[OTB])
        p.dma(DMA(ot_o[h * 64:(h + 1) * 64, :], OT[0:64, :]), reads=[OTB])
    p.finish()
    p.emit()
    return k.nc


def build_PMLA(S2):
    k = K()
    p = k.p
    D = D_MODEL
    NT = S2 // 512
    QL, KVL, R = 384, 256, 32
    h_in = k.din("h_in", [S2, D], F32)
    g_mix = k.din("g_mix", [1, D], F32)
    ident_d = k.din("ident", [128, 128], BF16)
    w_in = k.din("w_in", [D, 672], F32)
    gq_d = k.din("gq", [1, QL], F32)
    gkv_d = k.din("gkv", [1, KVL], F32)
    w_uq = k.din("w_uq", [QL, 1536], F32)
    w_uqs = k.din("w_uqs", [QL, 1536], F32)
    w_uk = k.din("w_uk", [KVL, 1024], F32)
    w_uv = k.din("w_uv", [KVL, 1024], F32)
    cosr_d = k.din("cos_r", [S2, 16], F32)
    sinr_d = k.din("sin_r", [S2, 16], F32)
    cosq_d = k.din("cosq", [128, S2], F32)
    sinq_d = k.din("sinq", [128, S2], F32)
    qt_o = k.dout("qt", [16 * 96, S2], BF16)
    knt_o = k.dout("knt", [1024, S2], BF16)
    krt_o = k.dout("krt", [32, S2], BF16)
    v_o = k.dout("v", [S2, 1024], BF16)

    banks = k.banks()
    ident, IDB = k.load_const(ident_d, [128, 128], BF16, "ident")
    g_rep, GB = k.load_const(g_mix.partition_broadcast(128), [128, D], F32, "g_rep")
    gq, GQB = k.load_const(gq_d.partition_broadcast(128), [128, QL], F32, "gq")
    gkv, GKB = k.load_const(gkv_d.partition_broadcast(128), [128, KVL], F32, "gkv")
    stgs = k.sbn("stg", 4, [128, 512], F32)
    win, WINB = k.sb("win", [128, 8, 672], BF16)
    k.load_w(w_in, D, 672, win, WINB, stgs)
    wuq, WUQB = k.sb("wuq", [128, 3, 1536], BF16)
    k.load_w(w_uq, QL, 1536, wuq, WUQB, stgs)
    wuqs, WUQSB = k.sb("wuqs", [128, 3, 1536], BF16)
    k.load_w(w_uqs, QL, 1536, wuqs, WUQSB, stgs)
    wuk, WUKB = k.sb("wuk", [128, 2, 1024], BF16)
    k.load_w(w_uk, KVL, 1024, wuk, WUKB, stgs)
    wuv, WUVB = k.sb("wuv", [128, 2, 1024], BF16)
    k.load_w(w_uv, KVL, 1024, wuv, WUVB, stgs)
    hs = k.sbn("h", 3, [128, D], F32)
    hns = k.sbn("hn", 2, [128, D], BF16)
    hnTs = k.sbn("hnT", 2, [128, 8, 512], BF16)
    junk, JB = k.sb("junk", [128, D], BF16)
    nstat = NT * 4 * 4
    ss, SSB = k.sb("ss", [128, nstat], F32)
    rr, _ = k.sb("rr", [128, nstat], F32)
    p.op("dve", lambda e: e.memset(ss[:], 0.0), writes=[SSB])
    cqns = k.sbn("cqn", 2, [128, QL], BF16)
    ckvns = k.sbn("ckvn", 2, [128, KVL], BF16)
    krrs = k.sbn("krr", 2, [128, R], BF16)
    tmps = k.sbn("tmp", 2, [128, 64], F32)
    cqnTs = k.sbn("cqnT", 2, [128, 3, 512], BF16)
    ckvnTs = k.sbn("ckvnT", 2, [128, 2, 512], BF16)
    krTs = k.sbn("krT", 2, [32, 512], BF16)
    crs = k.sbn("cr", 2, [128, 4, 16], F32)
    srs = k.sbn("sr", 2, [128, 4, 16], F32)
    cqs = k.sbn("cq", 2, [128, 512], F32)
    sqs = k.sbn("sq", 2, [128, 512], F32)
    t1s = k.sbn("t1", 2, [128, 512], F32)
    t2s = k.sbn("t2", 2, [128, 512], F32)
    qsts = k.sbn("qst", 3, [128, 512], BF16)
    ksts = k.sbn("kst", 2, [128, 8, 512], BF16)
    vsts = k.sbn("vst", 2, [128, 1024], BF16)
    for t in range(NT):
        hnT, HNTB = k.nxt("hnT", hnTs)
        cqnT, CQTB = k.nxt("cqnT", cqnTs)
        ckvnT, CKTB = k.nxt("ckvnT", ckvnTs)
        krT, KRTB = k.nxt("krT", krTs)
        cr, CRB = k.nxt("cr", crs)
        sr, SRB = k.nxt("sr", srs)
        p.dma(DMA(cr[:], cosr_d[t * 512:(t + 1) * 512, :].rearrange("(st q) f -> q st f", q=128)), writes=[CRB])
        p.dma(DMA(sr[:], sinr_d[t * 512:(t + 1) * 512, :].rearrange("(st q) f -> q st f", q=128)), writes=[SRB])
        for st in range(4):
            h, HB = k.nxt("h", hs)
            r0 = t * 512 + st * 128
            p.dma(DMA(h[:], h_in[r0:r0 + 128, :]), writes=[HB])
            hn, HNB = k.nxt("hn", hns)
            sc0 = (t * 4 + st) * 4
            rms_front(k, h, HB, ss, rr, SSB, sc0, g_rep, GB, hn, HNB, junk, JB)
            bank, BB = k.nxt("pa", banks)
            transpose_to(k, hn, HNB, 8, ident, IDB, bank, BB, hnT[:, :, st * 128:(st + 1) * 128], HNTB)
            cA, CAB = k.nxt("pa", banks)
            cB, CBB = k.nxt("pa", banks)
            for kc in range(8):
                p.op("pe", MM(cA[:, 0:512], hnT[:, kc, st * 128:(st + 1) * 128], win[:, kc, 0:512], kc == 0, kc == 7),
                     reads=[HNTB, WINB], writes=[CAB])
            for kc in range(8):
                p.op("pe", MM(cB[:, 0:160], hnT[:, kc, st * 128:(st + 1) * 128], win[:, kc, 512:672], kc == 0, kc == 7),
                     reads=[HNTB, WINB], writes=[CBB])
            cq, CQB = k.nxt("cqn", cqns)
            ckv, CKB = k.nxt("ckvn", ckvns)
            c1, c2, c3 = sc0 + 1, sc0 + 2, sc0 + 3
            p.op("act", ACTF(junk[:, 0:QL], cA[:, 0:QL], AF.Square, accum_out=ss[:, c1:c1 + 1]), reads=[CAB], writes=[JB, SSB])
            p.op("act", ACTF(rr[:, c1:c1 + 1], ss[:, c1:c1 + 1], AF.Sqrt, bias=RMS_EPS, scale=1.0 / QL), reads=[SSB], writes=[SSB])
            p.op("dve", (lambda c1: lambda e: e.reciprocal(rr[:, c1:c1 + 1], rr[:, c1:c1 + 1]))(c1), reads=[SSB], writes=[SSB])
            p.op("dve", STT(cq[:, :], cA[:, 0:QL], rr[:, c1:c1 + 1], gq[:, :], ALU.mult, ALU.mult), reads=[CAB, SSB, GQB], writes=[CQB])
            p.op("act", ACTF(junk[:, 0:128], cA[:, 384:512], AF.Square, accum_out=ss[:, c2:c2 + 1]), reads=[CAB], writes=[JB, SSB])
            p.op("act", ACTF(junk[:, 0:128], cB[:, 0:128], AF.Square, accum_out=ss[:, c3:c3 + 1]), reads=[CBB], writes=[JB, SSB])
            p.op("dve", TT(ss[:, c2:c2 + 1], ss[:, c2:c2 + 1], ss[:, c3:c3 + 1], ALU.add), reads=[SSB], writes=[SSB])
            p.op("act", ACTF(rr[:, c2:c2 + 1], ss[:, c2:c2 + 1], AF.Sqrt, bias=RMS_EPS, scale=1.0 / KVL), reads=[SSB], writes=[SSB])
            p.op("dve", (lambda c2: lambda e: e.reciprocal(rr[:, c2:c2 + 1], rr[:, c2:c2 + 1]))(c2), reads=[SSB], writes=[SSB])
            p.op("dve", STT(ckv[:, 0:128], cA[:, 384:512], rr[:, c2:c2 + 1], gkv[:, 0:128], ALU.mult, ALU.mult),
                 reads=[CAB, SSB, GKB], writes=[CKB])
            p.op("dve", STT(ckv[:, 128:256], cB[:, 0:128], rr[:, c2:c2 + 1], gkv[:, 128:256], ALU.mult, ALU.mult),
                 reads=[CBB, SSB, GKB], writes=[CKB])
            krr, KRB = k.nxt("krr", krrs)
            tm, TMB = k.nxt("tmp", tmps)
            x1, x2 = cB[:, 128:144], cB[:, 144:160]
            p.op("dve", TT(tm[:, 0:16], x1, cr[:, st, :], ALU.mult), reads=[CBB, CRB], writes=[TMB])
            p.op("dve", TT(tm[:, 16:32], x2, sr[:, st, :], ALU.mult), reads=[CBB, SRB], writes=[TMB])
            p.op("dve", TT(tm[:, 32:48], x2, cr[:, st, :], ALU.mult), reads=[CBB, CRB], writes=[TMB])
            p.op("dve", TT(tm[:, 48:64], x1, sr[:, st, :], ALU.mult), reads=[CBB, SRB], writes=[TMB])
            p.op("dve", TT(krr[:, 0:16], tm[:, 0:16], tm[:, 16:32], ALU.subtract), reads=[TMB], writes=[KRB])
            p.op("dve", TT(krr[:, 16:32], tm[:, 32:48], tm[:, 48:64], ALU.add), reads=[TMB], writes=[KRB])
            bank, BB = k.nxt("pa", banks)
            transpose_to(k, cq, CQB, 3, ident, IDB, bank, BB, cqnT[:, :, st * 128:(st + 1) * 128], CQTB)
            bank, BB = k.nxt("pa", banks)
            transpose_to(k, ckv, CKB, 2, ident, IDB, bank, BB, ckvnT[:, :, st * 128:(st + 1) * 128], CKTB)
            bank, BB = k.nxt("pa", banks)
            bv = bank[:].bitcast(BF16)
            p.op("pe", (lambda o, i: lambda e: e.transpose(o, i, ident[:]))(bv[0:32, 0:128], krr[:, :]), reads=[KRB, IDB], writes=[BB])
            p.op("act", COPY_ACT(krT[0:32, st * 128:(st + 1) * 128], bv[0:32, 0:128]), reads=[BB], writes=[KRTB])
        p.dma(DMA(krt_o[:, t * 512:(t + 1) * 512], krT[0:32, :]), reads=[KRTB])
        cqt, CQTT = k.nxt("cq", cqs)
        sqt, SQTT = k.nxt("sq", sqs)
        p.dma(DMA(cqt[:], cosq_d[:, t * 512:(t + 1) * 512]), writes=[CQTT])
        p.dma(DMA(sqt[:], sinq_d[:, t * 512:(t + 1) * 512]), writes=[SQTT])
        for hh in range(16):
            b1, B1 = k.nxt("pa", banks)
            b2, B2 = k.nxt("pa", banks)
            for kc in range(3):
                p.op("pe", MM(b1[0:96, :], wuq[:, kc, hh * 96:(hh + 1) * 96], cqnT[:, kc, :], kc == 0, kc == 2),
                     reads=[WUQB, CQTB], writes=[B1])
            for kc in range(3):
                p.op("pe", MM(b2[0:96, :], wuqs[:, kc, hh * 96:(hh + 1) * 96], cqnT[:, kc, :], kc == 0, kc == 2),
                     reads=[WUQSB, CQTB], writes=[B2])
            qst, QSB = k.nxt("qst", qsts)
            p.op("act", COPY_ACT(qst[0:64, :], b1[0:64, :]), reads=[B1], writes=[QSB])
            t1, T1B = k.nxt("t1", t1s)
            t2, T2B = k.nxt("t2", t2s)
            p.op("dve", TT(t1[64:96, :], b1[64:96, :], cqt[64:96, :], ALU.mult), reads=[B1, CQTT], writes=[T1B])
            p.op("dve", TT(t2[64:96, :], b2[64:96, :], sqt[64:96, :], ALU.mult), reads=[B2, SQTT], writes=[T2B])
            p.op("pool", TT(qst[64:96, :], t1[64:96, :], t2[64:96, :], ALU.add), reads=[T1B, T2B], writes=[QSB])
            p.dma(DMA(qt_o[hh * 96:(hh + 1) * 96, t * 512:(t + 1) * 512], qst[0:96, :]), reads=[QSB])
        kst, KSTB = k.nxt("kst", ksts)
        for hp in range(8):
            bank, BB = k.nxt("pa", banks)
            for kc in range(2):
                p.op("pe", MM(bank[:, :], wuk[:, kc, hp * 128:(hp + 1) * 128], ckvnT[:, kc, :], kc == 0, kc == 1),
                     reads=[WUKB, CKTB], writes=[BB])
            p.op("act", COPY_ACT(kst[:, hp, :], bank[:, :]), reads=[BB], writes=[KSTB])
        p.dma(DMA(knt_o.rearrange("(hp q) t -> q hp t", q=128)[:, :, t * 512:(t + 1) * 512], kst[:, :, :]), reads=[KSTB])
        for st in range(4):
            vst, VSB = k.nxt("vst", vsts)
            for cg in range(2):
                bank, BB = k.nxt("pa", banks)
                for kc in range(2):
                    p.op("pe", MM(bank[:, :], ckvnT[:, kc, st * 128:(st + 1) * 128], wuv[:, kc, cg * 512:(cg + 1) * 512],
                                  kc == 0, kc == 1), reads=[WUVB, CKTB], writes=[BB])
                p.op("act", COPY_ACT(vst[:, cg * 512:(cg + 1) * 512], bank[:, :]), reads=[BB], writes=[VSB])
            r0 = t * 512 + st * 128
            p.dma(DMA(v_o[r0:r0 + 128, :], vst[:, :]), reads=[VSB])
    p.finish()
    p.emit()
    return k.nc


_PROGS = {}
DEBUG = {}
CORES = list(range(8))


def _prog(key, fn):
    if key not in _PROGS:
        _PROGS[key] = fn()
    return _PROGS[key]


def _run(nc, maps):
    res = run_bass_kernel_spmd(nc, maps, core_ids=CORES)
    return res.results


def _rope_tables(S, dim):
    inv = (np.float32(10000.0) ** (-np.arange(0, dim, 2, dtype=np.float32) / np.float32(dim))).astype(np.float32)
    ang = (np.arange(S, dtype=np.float32)[:, None] * inv[None, :]).astype(np.float32)
    return np.cos(ang).astype(np.float32), np.sin(ang).astype(np.float32)


def _consts():
    i = np.arange(128)
    c = {}
    c["ident"] = np.eye(128, dtype=np.float32).astype(NPBF)
    c["tri"] = (i[None, :] >= i[:, None]).astype(np.float32).astype(NPBF)
    c["tris"] = (i[None, :] > i[:, None]).astype(np.float32).astype(NPBF)
    c["triu"] = (i[:, None] > i[None, :]).astype(np.float32).astype(NPBF)
    tri2 = (i[None, :] <= i[:, None]).astype(np.float32)
    c["band"] = np.concatenate([tri2, (i[None, :] >= i[:, None]).astype(np.float32)], axis=1).astype(NPBF)
    return c


def _swap_cols(w, nheads, hd, off, rd):
    w2 = w.copy()
    for h in range(nheads):
        a = h * hd + off
        w2[:, a:a + rd // 2] = w[:, a + rd // 2:a + rd]
        w2[:, a + rd // 2:a + rd] = w[:, a:a + rd // 2]
    return w2


def _v_layout(v, S):
    nh = v.shape[1] // 64
    NB = S // 128
    return np.ascontiguousarray(v.reshape(NB, 128, nh, 64).transpose(2, 1, 0, 3).reshape(nh, 128, NB * 64))


def kernel(x, norm_mix, norm_ffn, norm_final, a_w_qkv, a_w_o, b_w_in, b_q_norm, b_w_uq, b_kv_norm, b_w_ukv, b_w_o,
           c_w_qkv, c_w_o, d_w_qkv, d_w_o, ffn_w_gate, ffn_conv_w, ffn_conv_b, ffn_w_up, ffn_w_down,
           n_layers=4, sb_window=None):
    x = np.asarray(x, dtype=np.float32)
    B, S, D = x.shape
    S2 = S // 2
    f32 = lambda a: np.ascontiguousarray(np.asarray(a, dtype=np.float32))
    cst = _consts()
    cos_h, sin_h = _rope_tables(S, 64)
    cos_r, sin_r = _rope_tables(S, 32)
    pidx = np.arange(128)
    cosT = np.ascontiguousarray(cos_h[:, pidx % 32].T)
    sgn = np.where((pidx % 64) < 32, -1.0, 1.0).astype(np.float32)
    sinT = np.ascontiguousarray((sin_h[:, pidx % 32] * sgn[None, :]).T.astype(np.float32))
    jj = (pidx - 64) % 32
    cosq = np.ascontiguousarray(cos_r[:, jj % 16].T)
    sgq = np.where(jj < 16, -1.0, 1.0).astype(np.float32)
    sinq = np.ascontiguousarray((sin_r[:, jj % 16] * sgq[None, :]).T.astype(np.float32))
    h = x.copy()
    cores = [(b, t) for b in range(B) for t in range(2)]
    for li in range(n_layers):
        kind = li % 4
        if kind in (0, 2, 3):
            wqkv = f32((a_w_qkv, None, c_w_qkv, d_w_qkv)[kind][0])
            ngroups = 3 if kind == 2 else 1
            rope = kind != 3
            nc = _prog(("P", S2, rope), lambda: build_P(S2, rope))
            QT, KT, V = [], [], []
            for g in range(ngroups):
                base = g * 3072
                wq, wk, wv = (np.ascontiguousarray(wqkv[:, base + j * 1024:base + (j + 1) * 1024]) for j in range(3))
                maps = []
                for (b, t) in cores:
                    m = {"h_in": h[b, t * S2:(t + 1) * S2], "g_mix": f32(norm_mix[li])[None, :], "ident": cst["ident"],
                         "w_q": wq, "w_k": wk, "w_v": wv}
                    if rope:
                        m["w_qs"] = _swap_cols(wq, 16, 64, 0, 64)
                        m["w_ks"] = _swap_cols(wk, 16, 64, 0, 64)
                        m["cosT"] = np.ascontiguousarray(cosT[:, t * S2:(t + 1) * S2])
                        m["sinT"] = np.ascontiguousarray(sinT[:, t * S2:(t + 1) * S2])
                    maps.append(m)
                res = _run(nc, maps)
                QT.append([np.concatenate([res[2 * b]["qt"], res[2 * b + 1]["qt"]], axis=1) for b in range(B)])
                KT.append([np.concatenate([res[2 * b]["kt"], res[2 * b + 1]["kt"]], axis=1) for b in range(B)])
                V.append([np.concatenate([res[2 * b]["v"], res[2 * b + 1]["v"]], axis=0) for b in range(B)])
        else:
            nc = _prog(("PMLA", S2), lambda: build_PMLA(S2))
            w_ukv = f32(b_w_ukv[0]).reshape(256, 16, 2, 64)
            w_uk = np.ascontiguousarray(w_ukv[:, :, 0, :].reshape(256, 1024))
            w_uv = np.ascontiguousarray(w_ukv[:, :, 1, :].reshape(256, 1024))
            w_uq = f32(b_w_uq[0])
            maps = []
            for (b, t) in cores:
                maps.append({"h_in": h[b, t * S2:(t + 1) * S2], "g_mix": f32(norm_mix[li])[None, :], "ident": cst["ident"],
                             "w_in": f32(b_w_in[0]), "gq": f32(b_q_norm[0])[None, :], "gkv": f32(b_kv_norm[0])[None, :],
                             "w_uq": w_uq, "w_uqs": _swap_cols(w_uq, 16, 96, 64, 32), "w_uk": w_uk, "w_uv": w_uv,
                             "cos_r": np.ascontiguousarray(cos_r[t * S2:(t + 1) * S2]),
                             "sin_r": np.ascontiguousarray(sin_r[t * S2:(t + 1) * S2]),
                             "cosq": np.ascontiguousarray(cosq[:, t * S2:(t + 1) * S2]),
                             "sinq": np.ascontiguousarray(sinq[:, t * S2:(t + 1) * S2])})
            res = _run(nc, maps)
            cat = lambda n, ax: [np.concatenate([res[2 * b][n], res[2 * b + 1][n]], axis=ax) for b in range(B)]
            QTm, KNT, KRT, Vm = cat("qt", 1), cat("knt", 1), cat("krt", 1), cat("v", 0)
        if kind == 0 or kind == 3:
            if kind == 0:
                nc = _prog(("A", S, 64, True), lambda: build_A(S, 64, True))
            else:
                nc = _prog(("SB", S, sb_window), lambda: build_SB(S, sb_window))
            maps = []
            for (b, hh) in cores:
                m = {"qt": np.ascontiguousarray(QT[0][b][hh * 512:(hh + 1) * 512]),
                     "kt": np.ascontiguousarray(KT[0][b][hh * 512:(hh + 1) * 512]),
                     "v": _v_layout(V[0][b][:, hh * 512:(hh + 1) * 512], S), "tri": cst["tri"]}
                if kind == 0:
                    nblk = S // 256
                    ee = np.zeros((nblk, nblk, 128), np.float32)
                    ee[np.arange(nblk), np.arange(nblk), :] = 1.0
                    m["ee"] = ee.reshape(nblk, nblk * 128).astype(NPBF)
                    m["ident"] = cst["ident"]
                else:
                    m["tris"] = cst["tris"]
                    m["triu"] = cst["triu"]
                maps.append(m)
        elif kind == 1:
            nc = _prog(("A", S, 96, False), lambda: build_A(S, 96, False))
            maps = []
            for (b, hh) in cores:
                q = QTm[b][hh * 768:(hh + 1) * 768]
                kn = KNT[b][hh * 512:(hh + 1) * 512].reshape(8, 64, S)
                kt = np.concatenate([kn, np.broadcast_to(KRT[b][None], (8, 32, S))], axis=1).reshape(768, S)
                maps.append({"qt": np.ascontiguousarray(q), "kt": np.ascontiguousarray(kt),
                             "v": _v_layout(Vm[b][:, hh * 512:(hh + 1) * 512], S), "tri": cst["tri"]})
        else:
            nc = _prog(("DIL", S), lambda: build_DIL(S))
            perms = []
            for (window, d) in DIL_GROUPS:
                L = S // d
                perms.append((np.arange(L)[None, :] * d + np.arange(d)[:, None]).reshape(-1))
            maps = []
            for (b, hh) in cores:
                q = np.concatenate([QT[g][b][hh * 512:(hh + 1) * 512][:, perms[g]] for g in range(3)], axis=0)
                kk = np.concatenate([KT[g][b][hh * 512:(hh + 1) * 512][:, perms[g]] for g in range(3)], axis=0)
                vv = np.concatenate([_v_layout(V[g][b][perms[g]][:, hh * 512:(hh + 1) * 512], S) for g in range(3)], axis=0)
                maps.append({"qt": np.ascontiguousarray(q), "kt": np.ascontiguousarray(kk), "v": vv,
                             "tri": cst["tri"], "band": cst["band"]})
        res = _run(nc, maps)
        OT = [np.concatenate([res[2 * b]["ot"], res[2 * b + 1]["ot"]], axis=0) for b in range(B)]
        if DEBUG.get("on"):
            DEBUG[f"ot{li}"] = [o.copy() for o in OT]
        last = li == n_layers - 1
        nc = _prog(("F", S2, last), lambda: build_F(S2, last))
        w_o = f32((a_w_o, b_w_o, c_w_o, d_w_o)[kind][0])
        cw = f32(ffn_conv_w[li])
        cb = f32(ffn_conv_b[li])
        cwb = np.stack([cw[0], cw[1], cw[2], cb], axis=1).reshape(NFC, 128, 4).transpose(1, 0, 2).reshape(128, NFC * 4)
        maps = []
        for (b, t) in cores:
            if t == 0:
                hin = np.concatenate([np.zeros((128, D), np.float32), h[b, 0:S2]], axis=0)
                oin = np.concatenate([np.zeros((D, 128), NPBF), OT[b][:, 0:S2]], axis=1)
            else:
                hin = h[b, S2 - 128:S]
                oin = OT[b][:, S2 - 128:S]
            m = {"h_in": np.ascontiguousarray(hin), "ot_in": np.ascontiguousarray(oin), "w_o": w_o,
                 "w_gate": f32(ffn_w_gate[li]), "w_up": f32(ffn_w_up[li]), "w_down": f32(ffn_w_down[li]),
                 "g_ffn": f32(norm_ffn[li])[None, :], "cwb": np.ascontiguousarray(cwb), "ident": cst["ident"]}
            if last:
                m["g_fin"] = f32(norm_final)[None, :]
            maps.append(m)
        res = _run(nc, maps)
        h = np.stack([np.concatenate([res[2 * b]["h_out"], res[2 * b + 1]["h_out"]], axis=0) for b in range(B)], axis=0)
        if DEBUG.get("on"):
            DEBUG[f"h{li}"] = h.copy()
    return h.astype(np.float32)
```
